# Optimizing a Trainium2 kernel written in Bass

```python
import jax
import jax.numpy as jnp
from jax import lax
import numpy as np

D_MODEL = 1024
BATCH = 8
SEQ = 2048
DEPTH = 2

CTX_LEN = 256
GRID_W = 64
N_BRANCH = 4
BR_W = D_MODEL // 2
HEAD_DIM = 128
A_HEADS = BR_W // HEAD_DIM
A_KV_HEADS = A_HEADS // 2
A_GROUP = A_HEADS // A_KV_HEADS
Q_BLOCK = 128
ROPE_THETA = 10000.0
AXIS_DIM = HEAD_DIM // 2
B_CONV = 3
C_HEADS = 4
C_HEAD_K = BR_W // (2 * C_HEADS)
C_HEAD_V = BR_W // C_HEADS
C_KEY_W = C_HEADS * C_HEAD_K
C_GATE_RANK = 16
C_GATE_TAU = 16.0
C_CHUNK = 64
D_CONV = 31
ALPHA = (2 * DEPTH) ** 0.25
BETA = (8 * DEPTH) ** -0.25
EPS = 1e-6
IN_WIDTHS = (
    A_HEADS * HEAD_DIM, A_KV_HEADS * HEAD_DIM, A_KV_HEADS * HEAD_DIM, BR_W,
    BR_W, BR_W, BR_W, BR_W,
    C_KEY_W, C_KEY_W, C_HEADS * C_HEAD_V, BR_W, 2 * C_GATE_RANK,
    2 * BR_W, BR_W,
    N_BRANCH * D_MODEL,
)

kernel_name = 'hybrid_parallel_branch_diffusion_block'


def layer_norm(x):
    xf = x.astype(jnp.float32)
    mu = jnp.mean(xf, -1, keepdims=True)
    var = jnp.mean(jnp.square(xf - mu), -1, keepdims=True)
    return ((xf - mu) * lax.rsqrt(var + EPS)).astype(x.dtype)


def rms_norm(x, g):
    xf = x.astype(jnp.float32)
    y = xf * lax.rsqrt(jnp.mean(jnp.square(xf), -1, keepdims=True) + EPS)
    return y.astype(x.dtype) * g


def split_cols(w):
    cuts = [int(i) for i in np.cumsum(IN_WIDTHS)[:-1]]
    return jnp.split(w, cuts, axis=-1)


def dwconv(x, w):
    k, ch = w.shape
    return lax.conv_general_dilated(x, w[:, None, :].astype(x.dtype), (1,), [(k // 2, k // 2)],
                                    dimension_numbers=('NWC', 'WIO', 'NWC'), feature_group_count=ch)


def apply_rope(x, cos, sin):
    xp = x.reshape(x.shape[:-1] + (HEAD_DIM // 2, 2))
    x0, x1 = xp[..., 0], xp[..., 1]
    c = cos[:, None, :].astype(x.dtype)
    s = sin[:, None, :].astype(x.dtype)
    return jnp.stack([x0 * c - x1 * s, x0 * s + x1 * c], -1).reshape(x.shape)


def attn_heads(q, k, v, q_g, k_g, cos, sin):
    b, t, _ = q.shape
    qh = rms_norm(q.reshape(b, t, A_HEADS, HEAD_DIM), q_g)
    kh = rms_norm(k.reshape(b, t, A_KV_HEADS, HEAD_DIM), k_g)
    if cos is not None:
        qh = apply_rope(qh, cos, sin)
        kh = apply_rope(kh, cos, sin)
    qh = qh.reshape(b, t, A_KV_HEADS, A_GROUP, HEAD_DIM).transpose(0, 2, 3, 1, 4)
    kh = kh.transpose(0, 2, 1, 3)
    vh = v.reshape(b, t, A_KV_HEADS, HEAD_DIM).transpose(0, 2, 1, 3)
    return qh, kh, vh


def softmax_attention(q, k, v):
    s = jnp.einsum('bkgqd,bksd->bkgqs', q, k).astype(jnp.float32) * HEAD_DIM ** -0.5
    p = jax.nn.softmax(s, axis=-1).astype(v.dtype)
    return jnp.einsum('bkgqs,bksd->bkgqd', p, v)


def block_attention(q, k, v):
    b, kv, g, t, hd = q.shape
    nb = t // Q_BLOCK
    qb = q.reshape(b, kv, g, nb, Q_BLOCK, hd).transpose(3, 0, 1, 2, 4, 5)
    o = lax.map(lambda qi: softmax_attention(qi, k, v), qb)
    return o.transpose(1, 0, 4, 2, 3, 5).reshape(b, t, kv * g * hd)


def merge_heads(o):
    b, kv, g, t, hd = o.shape
    return o.transpose(0, 3, 1, 2, 4).reshape(b, t, kv * g * hd)


def gla_chunked(q, k, v, g, s0, want_out):
    b, h, l, _ = q.shape
    dv = v.shape[-1]
    n = l // C_CHUNK

    def chunks(a):
        return a.astype(jnp.float32).reshape(b, h, n, C_CHUNK, a.shape[-1]).transpose(2, 0, 1, 3, 4)

    mask = jnp.tril(jnp.ones((C_CHUNK, C_CHUNK), dtype=bool))[:, :, None]

    def step(s, inp):
        qc, kc, vc, gc = inp
        cum = jnp.cumsum(gc, axis=2)
        last = cum[:, :, -1:, :]
        s_new = jnp.exp(last[:, :, 0, :])[..., None] * s + jnp.einsum('bhjd,bhjv->bhdv', kc * jnp.exp(last - cum), vc)
        if not want_out:
            return s_new, None
        diff = cum[:, :, :, None, :] - cum[:, :, None, :, :]
        decay = jnp.exp(jnp.where(mask, diff, -jnp.inf))
        scores = jnp.einsum('bhid,bhjd,bhijd->bhij', qc, kc, decay)
        o = jnp.einsum('bhij,bhjv->bhiv', scores, vc) + jnp.einsum('bhid,bhdv->bhiv', qc * jnp.exp(cum), s)
        return s_new, o

    s_fin, o = lax.scan(step, s0, (chunks(q), chunks(k), chunks(v), chunks(g)))
    if not want_out:
        return None, s_fin
    return o.transpose(1, 2, 0, 3, 4).reshape(b, h, l, dv).astype(v.dtype), s_fin


def gla_branch(side_c, side_l, w2, b2, norm_g, want_ctx):
    def heads(a, d):
        return a.reshape(a.shape[0], a.shape[1], C_HEADS, d).transpose(0, 2, 1, 3)

    def prep(q, k, v, r):
        gates = [heads(jax.nn.log_sigmoid((r[..., i * C_GATE_RANK:(i + 1) * C_GATE_RANK] @ w2[i] + b2[i])
                                          .astype(jnp.float32)) / C_GATE_TAU, C_HEAD_K) for i in range(2)]
        return heads(q, C_HEAD_K) * C_HEAD_K ** -0.5, heads(k, C_HEAD_K), heads(v, C_HEAD_V), gates[0], gates[1]

    def flip(a):
        return jnp.flip(a, axis=2)

    qc, kc, vc, gcf, gcb = prep(*side_c)
    ql, kl, vl, glf, glb = prep(*side_l)
    s0 = jnp.zeros((qc.shape[0], C_HEADS, C_HEAD_K, C_HEAD_V), jnp.float32)
    oc_f, sc_f = gla_chunked(qc, kc, vc, gcf, s0, want_ctx)
    oc_b, sc_b = gla_chunked(flip(qc), flip(kc), flip(vc), flip(gcb), s0, want_ctx)
    ol_f, _ = gla_chunked(ql, kl, vl, glf, sc_f, True)
    ol_b, _ = gla_chunked(flip(ql), flip(kl), flip(vl), flip(glb), sc_b, True)

    def finish(of, ob):
        o = rms_norm(of + flip(ob), norm_g.reshape(C_HEADS, 1, C_HEAD_V))
        return o.transpose(0, 2, 1, 3).reshape(o.shape[0], o.shape[2], C_HEADS * C_HEAD_V)

    y_ctx = finish(oc_f, oc_b) if want_ctx else None
    return y_ctx, finish(ol_f, ol_b)


def short_conv(bg, cg, xin, z, w):
    return bg * dwconv(cg * xin, w) * jax.nn.silu(z)


def conformer_conv(glu, z, w, bias, g, beta):
    a, gt = jnp.split(glu, 2, axis=-1)
    hh = dwconv(a * jax.nn.sigmoid(gt), w) + bias
    hh = layer_norm(hh) * g + beta
    return jax.nn.silu(hh) * jax.nn.silu(z)


def merge(branches, mg, w_br, w_out):
    b, t, _ = mg.shape
    gates = jax.nn.sigmoid(mg.reshape(b, t, N_BRANCH, D_MODEL))
    acc = gates[:, :, 0] * (branches[0] @ w_br[0])
    for i in range(1, N_BRANCH):
        acc = acc + gates[:, :, i] * (branches[i] @ w_br[i])
    return acc @ w_out


def mixer(u_ctx, u_lat, cos, sin, w_in, q_g, k_g, b_w, c_w2, c_b2, c_g, d_w, d_b, d_g, d_beta, w_br, w_out, want_ctx):
    w_parts = split_cols(w_in)
    pc = [u_ctx @ w for w in w_parts]
    pl = [u_lat @ w for w in w_parts]
    qa_l, ka_l, va_l = attn_heads(pl[0], pl[1], pl[2], q_g, k_g, cos, sin)
    qa_c, ka_c, va_c = attn_heads(pc[0], pc[1], pc[2], q_g, k_g, None, None)
    ya_l = block_attention(qa_l, jnp.concatenate([ka_l, ka_c], 2), jnp.concatenate([va_l, va_c], 2)) * jax.nn.silu(pl[3])
    yc_c, yc_l = gla_branch((pc[8], pc[9], pc[10], pc[12]), (pl[8], pl[9], pl[10], pl[12]), c_w2, c_b2, c_g, want_ctx)
    yc_l = yc_l * jax.nn.silu(pl[11])

    def local_branches(p):
        yb = short_conv(p[4], p[5], p[6], p[7], b_w)
        yd = conformer_conv(p[13], p[14], d_w, d_b, d_g, d_beta)
        return yb, yd

    yb_l, yd_l = local_branches(pl)
    y_lat = merge([ya_l, yb_l, yc_l, yd_l], pl[15], w_br, w_out)
    if not want_ctx:
        return None, y_lat
    ya_c = merge_heads(softmax_attention(qa_c, ka_c, va_c)) * jax.nn.silu(pc[3])
    yb_c, yd_c = local_branches(pc)
    y_ctx = merge([ya_c, yb_c, yc_c * jax.nn.silu(pc[11]), yd_c], pc[15], w_br, w_out)
    return y_ctx, y_lat


def setup_inputs(seed: int = 0) -> dict:
    key = jax.random.key(seed)
    ks = jax.random.split(key, 24)

    def nrm(k, shape, s):
        return jax.random.normal(k, shape, jnp.float32) * s

    total_in = sum(IN_WIDTHS)
    return {
        'x': nrm(ks[0], (BATCH, SEQ, D_MODEL), 1.0),
        'c': nrm(ks[1], (BATCH, D_MODEL), 1.0),
        'ctx': nrm(ks[2], (BATCH, CTX_LEN, D_MODEL), 1.0),
        'c_ctx': nrm(ks[3], (D_MODEL,), 1.0),
        'w_mod': nrm(ks[4], (DEPTH, D_MODEL, 3 * D_MODEL), D_MODEL ** -0.5),
        'b_mod': nrm(ks[5], (DEPTH, 3 * D_MODEL), 0.02),
        'w_in': nrm(ks[6], (DEPTH, D_MODEL, total_in), D_MODEL ** -0.5),
        'q_norm': 1.0 + nrm(ks[7], (DEPTH, HEAD_DIM), 0.02),
        'k_norm': 1.0 + nrm(ks[8], (DEPTH, HEAD_DIM), 0.02),
        'b_conv': nrm(ks[9], (DEPTH, B_CONV, BR_W), B_CONV ** -0.5),
        'c_gate_w2': nrm(ks[10], (DEPTH, 2, C_GATE_RANK, C_KEY_W), C_GATE_RANK ** -0.5),
        'c_gate_b': nrm(ks[11], (DEPTH, 2, C_KEY_W), 0.1),
        'c_norm': 1.0 + nrm(ks[12], (DEPTH, BR_W), 0.02),
        'd_conv_w': nrm(ks[13], (DEPTH, D_CONV, BR_W), D_CONV ** -0.5),
        'd_conv_b': nrm(ks[14], (DEPTH, BR_W), 0.02),
        'd_norm_g': 1.0 + nrm(ks[15], (DEPTH, BR_W), 0.02),
        'd_norm_b': nrm(ks[16], (DEPTH, BR_W), 0.02),
        'w_br': nrm(ks[17], (DEPTH, N_BRANCH, BR_W, D_MODEL), BETA * BR_W ** -0.5),
        'w_out': nrm(ks[18], (DEPTH, D_MODEL, D_MODEL), BETA * D_MODEL ** -0.5),
        'ln_g': 1.0 + nrm(ks[19], (DEPTH, D_MODEL), 0.02),
        'ln_b': nrm(ks[20], (DEPTH, D_MODEL), 0.02),
    }


def reference(x, c, ctx, c_ctx, w_mod, b_mod, w_in, q_norm, k_norm, b_conv, c_gate_w2, c_gate_b, c_norm,
              d_conv_w, d_conv_b, d_norm_g, d_norm_b, w_br, w_out, ln_g, ln_b):
    rows = x.shape[1] // GRID_W
    row = jnp.repeat(jnp.arange(rows), GRID_W).astype(jnp.float32)
    col = jnp.tile(jnp.arange(GRID_W), rows).astype(jnp.float32)
    inv = ROPE_THETA ** (-jnp.arange(0, AXIS_DIM, 2, dtype=jnp.float32) / AXIS_DIM)
    ang = jnp.concatenate([row[:, None] * inv, col[:, None] * inv], -1)
    cos, sin = jnp.cos(ang), jnp.sin(ang)
    sc = jax.nn.silu(c)
    scc = jax.nn.silu(c_ctx)
    h_lat, h_ctx = x, ctx
    for l in range(DEPTH):
        want_ctx = l < DEPTH - 1
        shift, scale, gate = jnp.split(sc @ w_mod[l] + b_mod[l], 3, axis=-1)
        shift_c, scale_c, gate_c = jnp.split(scc @ w_mod[l] + b_mod[l], 3, axis=-1)
        u_lat = layer_norm(h_lat) * (1.0 + scale[:, None]) + shift[:, None]
        u_ctx = layer_norm(h_ctx) * (1.0 + scale_c) + shift_c
        y_ctx, y_lat = mixer(u_ctx, u_lat, cos, sin, w_in[l], q_norm[l], k_norm[l], b_conv[l], c_gate_w2[l],
                             c_gate_b[l], c_norm[l], d_conv_w[l], d_conv_b[l], d_norm_g[l], d_norm_b[l],
                             w_br[l], w_out[l], want_ctx)
        h_lat = layer_norm(ALPHA * h_lat + gate[:, None] * y_lat) * ln_g[l] + ln_b[l]
        if want_ctx:
            h_ctx = layer_norm(ALPHA * h_ctx + gate_c * y_ctx) * ln_g[l] + ln_b[l]
    return h_lat
```

```python
import numpy as np
import concourse.bass as bass
import concourse.mybir as mybir
from concourse.bass_utils import run_bass_kernel_spmd

F32 = mybir.dt.float32
BF16 = mybir.dt.bfloat16
AF = mybir.ActivationFunctionType
ALU = mybir.AluOpType
AX = mybir.AxisListType

D = 1024
SEQ = 2048
CTX = 256
T = SEQ + CTX
NT = T // 128
DEPTH = 2
ALPHA = (2 * DEPTH) ** 0.25
EPS = 1e-6
TOTAL_IN = 10784
O_QA, O_KA, O_VA, O_ZA = 0, 512, 768, 1024
O_BB, O_CB, O_XB, O_ZB = 1536, 2048, 2560, 3072
O_QC, O_KC, O_VC, O_ZC, O_R = 3584, 3840, 4096, 4608, 5120
O_GA, O_GT, O_ZD, O_MG = 5152, 5664, 6176, 6688
TBLK = [(0, 256)] + [(256 + 512 * j, 512) for j in range(4)]


class Buf:
    __slots__ = ("name", "last_w", "readers")

    def __init__(self, name):
        self.name = name
        self.last_w = None
        self.readers = {}


class Op:
    __slots__ = ("eng", "idx", "fn", "deps", "is_dma", "sem", "val", "target", "waits")

    def __init__(self, eng, idx, fn, is_dma):
        self.eng = eng
        self.idx = idx
        self.fn = fn
        self.deps = []
        self.is_dma = is_dma
        self.sem = None
        self.val = None
        self.target = False
        self.waits = []


class Sched:
    ENGS = ("pe", "act", "dve", "pool", "sp")
    NDMA_SEMS = 12

    def __init__(self, nc):
        self.nc = nc
        self.ops = {e: [] for e in self.ENGS}
        self.dma_last = {e: [None] * self.NDMA_SEMS for e in self.ENGS}
        self.dma_cnt = {e: 0 for e in self.ENGS}
        self.dma_uses = {e: [0] * self.NDMA_SEMS for e in self.ENGS}
        self.bufs = {}
        self.pending = {e: [] for e in self.ENGS}
        self.dma_since_bar = []

    def buf(self, name):
        b = self.bufs.get(name)
        if b is None:
            b = self.bufs[name] = Buf(name)
        return b

    mute = False

    def _mk(self, eng, fn, reads, writes, is_dma, extra=()):
        if self.mute:
            return None
        op = Op(eng, len(self.ops[eng]), fn, is_dma)
        deps = []
        rb = [self.buf(b) for b in reads]
        wb = [self.buf(b) for b in writes]
        for b in rb:
            if b.last_w is not None:
                deps.append(b.last_w)
        for b in wb:
            if b.last_w is not None:
                deps.append(b.last_w)
            deps.extend(b.readers.values())
        deps.extend(extra)
        if self.pending[eng]:
            deps.extend(self.pending[eng])
            self.pending[eng] = []
        key = ("d", eng, op.idx) if is_dma else eng
        for b in rb:
            b.readers[key] = op
        for b in wb:
            b.last_w = op
            b.readers = {}
        if is_dma:
            k = self.dma_cnt[eng] % self.NDMA_SEMS
            self.dma_cnt[eng] += 1
            prev = self.dma_last[eng][k]
            if prev is not None:
                deps.append(prev)
            self.dma_last[eng][k] = op
            self.dma_uses[eng][k] += 1
            op.sem = (eng, k)
            op.val = 16 * self.dma_uses[eng][k]
            self.dma_since_bar.append(op)
        op.deps = [d for d in deps if d is not op]
        self.ops[eng].append(op)
        return op

    def op(self, eng, fn, reads=(), writes=(), extra=()):
        return self._mk(eng, fn, reads, writes, False, extra)

    def dma(self, eng, fn, reads=(), writes=(), extra=()):
        return self._mk(eng, fn, reads, writes, True, extra)

    def barrier(self):
        last = []
        for e in self.ENGS:
            for o in reversed(self.ops[e]):
                if not o.is_dma:
                    last.append(o)
                    break
        last.extend(self.dma_since_bar)
        self.dma_since_bar = []
        for e in self.ENGS:
            self.pending[e] = list(self.pending[e]) + last

    def finalize_and_emit(self):
        nc = self.nc
        for e in self.ENGS:
            seen = {}
            seen_dma = set()
            for op in self.ops[e]:
                need = {}
                dneed = []
                for d in op.deps:
                    if d.is_dma:
                        if id(d) not in seen_dma:
                            seen_dma.add(id(d))
                            dneed.append(d)
                    else:
                        if d.eng == e and e == "pe":
                            continue
                        if d.idx <= seen.get(d.eng, -1):
                            continue
                        if d.idx > need.get(d.eng, (-1, None))[0]:
                            need[d.eng] = (d.idx, d)
                for en, (ix, d) in need.items():
                    seen[en] = ix
                    d.target = True
                op.waits = [d for (_, d) in need.values()] + dneed
        sems = {}
        for e in self.ENGS:
            sems[e] = nc.alloc_semaphore(name=f"s_{e}")
            c = 0
            for op in self.ops[e]:
                if op.is_dma:
                    continue
                if op.target:
                    c += 1
                    op.sem = ("c", e)
                    op.val = c
        dsems = {}
        for e in self.ENGS:
            for k in range(self.NDMA_SEMS):
                if self.dma_uses[e][k] > 0:
                    dsems[(e, k)] = nc.alloc_semaphore(name=f"d_{e}_{k}")

        def semof(op):
            if op.is_dma:
                return dsems[op.sem]
            return sems[op.sem[1]]

        def run(e, engobj):
            for op in self.ops[e]:
                for d in op.waits:
                    engobj.wait_ge(semof(d), d.val)
                ins = op.fn(engobj)
                if op.is_dma:
                    ins.then_inc(semof(op), 16)
                elif op.target:
                    ins.then_inc(semof(op), 1)

        with nc.Block() as block:
            @block.sync
            def _(eng):
                run("sp", eng)

            @block.tensor
            def _(eng):
                run("pe", eng)

            @block.scalar
            def _(eng):
                run("act", eng)

            @block.vector
            def _(eng):
                run("dve", eng)

            @block.gpsimd
            def _(eng):
                run("pool", eng)
        return {e: len(self.ops[e]) for e in self.ENGS}


class Arena:
    BASE = 16512
    LIMIT = 229376

    def __init__(self, nc):
        self.nc = nc
        self.top = self.BASE
        self.n = 0
        self.peak = self.BASE

    def alloc(self, name, shape, dtype):
        esz = 2 if dtype == BF16 else 4
        size = esz * int(np.prod(shape[1:]))
        off = (self.top + 31) // 32 * 32
        self.top = off + size
        self.peak = max(self.peak, self.top)
        assert self.top <= self.LIMIT, (name, self.top)
        self.n += 1
        return self.nc.alloc_sbuf_tensor_at(f"{name}_{self.n}", list(shape), dtype, offset=off)


def build_program(n_layers=DEPTH, dbg=(), phases="P0,P1,C,D,A,B,M,E", clim=99):
    nc = bass.Bass("TRN2", target_bir_lowering=False)
    S = Sched(nc)
    A = Arena(nc)
    dbg = set(dbg)
    phases = set(phases.split(","))
    dbg_out = {}

    def din(name, shape, dt=F32):
        return nc.dram_tensor(name, list(shape), dt, kind="ExternalInput").ap()

    x_d = din("x", [SEQ, D])
    ctx_d = din("ctx", [CTX, D])
    cpk_d = din("cpk", [128, 8, 2])
    w_mod_d = din("w_mod", [DEPTH, D, 3 * D])
    bmod_d = din("bmod", [DEPTH, 128, 24])
    w_in_d = din("w_in", [DEPTH, D, TOTAL_IN])
    qg_d = din("qg", [DEPTH, 128, 128])
    kg_d = din("kg", [DEPTH, 128, 128])
    bw_d = din("bw", [DEPTH, 128, 4, 3])
    w2c_d = din("w2c", [DEPTH, 128, 512])
    cn_d = din("cn", [DEPTH, 128, 512])
    dw_d = din("dw", [DEPTH, 128, 4, 31])
    dvec_d = din("dvec", [DEPTH, 128, 3, 4])
    w_br_d = din("w_br", [DEPTH, 4, 512, D])
    w_out_d = din("w_out", [DEPTH, D, D])
    lng_d = din("lng", [DEPTH, 128, D])
    lnb_d = din("lnb", [DEPTH, 128, D])
    cst_d = din("cst", [128, 5, 128])
    rope_d = din("rope", [128, 2, 16, 64])
    out_d = nc.dram_tensor("out", [SEQ, D], F32, kind="ExternalOutput").ap()
    hbuf = nc.dram_tensor("hbuf", [T, D], F32, kind="Internal").ap()

    def dump(name, src_ap, shape, reads):
        if name not in dbg:
            return
        t = nc.dram_tensor("dbg_" + name, list(shape), src_ap.dtype, kind="ExternalOutput").ap()
        dbg_out[name] = S.dma("sp", lambda e: e.dma_start(out=t, in_=src_ap), reads=reads)

    cst = A.alloc("cst", [128, 5, 128], F32)
    ident = cst[:, 0, :]
    ones_f = cst[:, 1, :]
    tri = [cst[:, 2, :], cst[:, 3, :]]
    cstb = A.alloc("cstb", [128, 2, 128], BF16)
    ident_b = cstb[:, 0, :]
    ones_b = cstb[:, 1, :]
    mask4 = A.alloc("mask4", [128, 2, 4, 128], F32)
    trib = A.alloc("trib", [128, 2, 128], BF16)
    cpk = A.alloc("cpk", [128, 8, 2], F32)
    scb = A.alloc("scb", [128, 8, 2], BF16)
    modT = A.alloc("modT", [128, 24, 2], F32)
    bmod = A.alloc("bmod", [128, 24], F32)
    qg = A.alloc("qg", [128, 128], F32)
    kg = A.alloc("kg", [128, 128], F32)
    bw = A.alloc("bw", [128, 4, 3], F32)
    dw = A.alloc("dw", [128, 4, 31], F32)
    dvec = A.alloc("dvec", [128, 3, 4], F32)
    cn = A.alloc("cn", [128, 512], F32)
    w2c = A.alloc("w2c", [128, 512], BF16)
    NWS = 4
    wsl = [A.alloc(f"ws{i}", [128, 8, 512], BF16) for i in range(NWS)]
    uT = A.alloc("uT", [128, 8, T], BF16)
    yT = [A.alloc(f"y{i}", [128, 4, T], BF16) for i in range(4)]
    Y_C, Y_D, Y_A, Y_B = 0, 1, 2, 3
    base_after_y = [None] * 4
    marks = {}
    ysz = 4 * T * 2
    y0_off = A.top - 4 * ysz
    marks["after_C"] = y0_off + 1 * ysz
    marks["after_CD"] = y0_off + 2 * ysz
    marks["after_CDA"] = y0_off + 3 * ysz
    marks["after_all"] = A.top
    marks["y0"] = y0_off

    ps = [nc.alloc_psum_tensor(f"ps{i}", [128, 512], F32) for i in range(8)]

    def P(i):
        return f"ps{i}"

    ws_ctr = [0]

    def WS(k):
        return [f"ws{k}.{pi}" for pi in range(4)]

    def wslot():
        k = ws_ctr[0] % NWS
        ws_ctr[0] += 1
        return k

    def load_cols(l, k, pieces):
        wv = w_in_d[l].rearrange("(kc p) n -> p kc n", p=128)
        for pi, (d0, s0, n) in enumerate(pieces):
            wr = [f"ws{k}.{pi}"] if len(pieces) > 1 else WS(k)
            S.dma("pool", lambda e, d0=d0, s0=s0, n=n: e.dma_start(out=wsl[k][:, :, d0:d0 + n], in_=wv[:, :, s0:s0 + n]),
                  writes=wr)

    def act(fn, reads, writes):
        return S.op("act", fn, reads, writes)

    def dve(fn, reads, writes):
        return S.op("dve", fn, reads, writes)

    def pe(fn, reads, writes):
        return S.op("pe", fn, reads, writes)

    def rsqrt_act(out_ap, in_ap, scale, rd, wr):
        act(lambda e: e.activation(out=out_ap, in_=in_ap, func=AF.Ln, bias=EPS, scale=scale), rd, wr)
        act(lambda e: e.activation(out=out_ap, in_=out_ap, func=AF.Exp, scale=-0.5), wr, wr)

    def uT_bufs(t0, n):
        return [f"uT{tt}" for tt in range(t0 // 128, (t0 + n) // 128)]

    S.dma("sp", lambda e: e.dma_start(out=cst[:], in_=cst_d), writes=["cst"])
    S.dma("sp", lambda e: e.dma_start(out=cpk[:], in_=cpk_d), writes=["cpk"])
    dve(lambda e: e.tensor_copy(out=cstb[:, 0, :], in_=cst[:, 0, :]), ["cst"], ["cstb"])
    dve(lambda e: e.tensor_copy(out=cstb[:, 1, :], in_=cst[:, 1, :]), ["cst"], ["cstb"])
    for d_ in range(2):
        for h in range(4):
            dve(lambda e, d_=d_, h=h: e.tensor_copy(out=mask4[:, d_, h, :], in_=cst[:, 2 + d_, :]), ["cst"], ["mask4"])
    act(lambda e: e.activation(out=scb[:], in_=cpk[:], func=AF.Silu), ["cpk"], ["scb"])
    dve(lambda e: e.tensor_copy(out=trib[:], in_=cst[:, 2:4, :]), ["cst"], ["trib"])

    def emit_layer(l):
        last = (l == DEPTH - 1)
        A.top = marks["after_all"]
        S.barrier()

        def hsrc(tt):
            if l == 0:
                return ctx_d[tt * 128:(tt + 1) * 128, :] if tt < 2 else x_d[(tt - 2) * 128:(tt - 1) * 128, :]
            return hbuf[tt * 128:(tt + 1) * 128, :]

        for (dst, src, nm) in ((bmod, bmod_d, "bmod"), (qg, qg_d, "qg"), (kg, kg_d, "kg"), (bw, bw_d, "bw"),
                               (dw, dw_d, "dw"), (dvec, dvec_d, "dvec"), (cn, cn_d, "cn")):
            S.dma("sp", lambda e, dst=dst, src=src: e.dma_start(out=dst[:], in_=src[l]), writes=[nm])
        S.dma("pool", lambda e: e.dma_start(out=w2c[:], in_=w2c_d[l]), writes=["w2c"])

        S.mute = "P0" not in phases
        wmv = w_mod_d[l].rearrange("(kc p) n -> p kc n", p=128)
        for cb in range(6):
            k = wslot()
            S.dma("pool", lambda e, k=k, cb=cb: e.dma_start(out=wsl[k][:], in_=wmv[:, :, cb * 512:(cb + 1) * 512]), writes=WS(k))
            for c4 in range(4):
                fch = cb * 4 + c4
                for kc in range(8):
                    pe(lambda e, k=k, c4=c4, fch=fch, kc=kc: e.matmul(ps[0][:, fch * 2:fch * 2 + 2], lhsT=wsl[k][:, kc, c4 * 128:(c4 + 1) * 128],
                                                                      rhs=scb[:, kc, :], start=(kc == 0), stop=(kc == 7)),
                       WS(k) + ["scb"], [P(0)])
        for w in range(2):
            dve(lambda e, w=w: e.tensor_tensor(out=modT[:, :, w], in0=ps[0][:, 0:48].rearrange("p (c w) -> p c w", w=2)[:, :, w],
                                               in1=bmod[:], op=ALU.add), [P(0), "bmod"], ["modT"])
        dve(lambda e: e.tensor_scalar_add(out=modT[:, 8:16, :], in0=modT[:, 8:16, :], scalar1=1.0), ["modT"], ["modT"])

        S.mute = "P1" not in phases
        m_l = A.top
        hin = [A.alloc(f"hin{i}", [128, D], F32) for i in range(2)]
        xn = [A.alloc(f"xn{i}", [128, D], F32) for i in range(2)]
        st = [A.alloc(f"st{i}", [128, 16], F32) for i in range(2)]
        for tt in range(NT):
            i = tt % 2
            w = 0 if tt >= 2 else 1
            S.dma("sp", lambda e, i=i, tt=tt: e.dma_start(out=hin[i][:], in_=hsrc(tt)), reads=[f"hb{tt}"], writes=[f"hin{i}"])
            for hh in range(2):
                dve(lambda e, i=i, hh=hh: e.bn_stats(out=st[i][:, hh * 6:(hh + 1) * 6], in_=hin[i][:, hh * 512:(hh + 1) * 512]), [f"hin{i}"], [f"st{i}"])
            dve(lambda e, i=i: e.bn_aggr(out=st[i][:, 12:14], in_=st[i][:, 0:12]), [f"st{i}"], [f"st{i}"])
            rsqrt_act(st[i][:, 14:15], st[i][:, 13:14], 1.0, [f"st{i}"], [f"st{i}"])
            dve(lambda e, i=i: e.tensor_scalar(out=xn[i][:], in0=hin[i][:], scalar1=st[i][:, 12:13], scalar2=st[i][:, 14:15],
                                               op0=ALU.subtract, op1=ALU.mult), [f"hin{i}", f"st{i}"], [f"xn{i}"])
            for g in range(2):
                pb = 1 + i * 2 + g
                for c in range(4):
                    kc = g * 4 + c
                    pe(lambda e, i=i, c=c, kc=kc, pb=pb: e.transpose(out=ps[pb][:, c * 128:(c + 1) * 128], in_=xn[i][:, kc * 128:(kc + 1) * 128], identity=ident),
                       [f"xn{i}", "cst"], [P(pb)])
                for c in range(4):
                    kc = g * 4 + c
                    act(lambda e, c=c, kc=kc, pb=pb, tt=tt, w=w: e.activation(out=uT[:, kc, tt * 128:(tt + 1) * 128], in_=ps[pb][:, c * 128:(c + 1) * 128],
                                                                           func=AF.Identity, bias=modT[:, kc, w:w + 1], scale=modT[:, 8 + kc, w:w + 1]),
                        [P(pb), "modT"], [f"uT{tt}"])
        dump("uT", uT[:], [128, 8, T], [f"uT{tt}" for tt in range(NT)])
        A.top = marks["after_C"]
        S.barrier()

        S.mute = "C" not in phases
        m_c = A.top
        rT = A.alloc("rT", [64, T], BF16)
        ostore = A.alloc("ostore", [128, NT, 512], F32)
        Sst = A.alloc("Sst", [128, 2, 128], F32)
        Sbf = A.alloc("Sbf", [128, 2, 128], BF16)
        lsp = A.alloc("lsp", [128, 256], F32)
        lhl = A.alloc("lhl", [128, 2, 256], BF16)
        eq = A.alloc("eq", [128, 256], F32)
        ek = A.alloc("ek", [128, 256], F32)
        dec = A.alloc("dec", [128, 2], F32)
        qs = A.alloc("qs", [128, 256], F32)
        kst = A.alloc("kst", [128, 256], F32)
        ktb = A.alloc("ktb", [128, 256], BF16)
        vbf = A.alloc("vbf", [128, 512], BF16)
        qkT = A.alloc("qkT", [128, 4, 128], BF16)
        Pm = A.alloc("Pm", [128, 4, 128], BF16)
        qm = A.alloc("qm", [128, 4, 128], BF16)
        tmpS = A.alloc("tmpS", [128, 2, 128], F32)
        osum = A.alloc("osum", [128, 512], F32)
        sq = A.alloc("sq", [128, 512], F32)
        ss = A.alloc("ss", [128, 8], F32)
        zs = A.alloc("zs", [128, 512], F32)
        og = A.alloc("og", [128, 512], F32)
        def cstep(n_):
            S.mute = ("C" not in phases) or (n_ > clim)
        kqk, kv_, kz, kr = wslot(), wslot(), wslot(), wslot()
        load_cols(l, kqk, [(0, O_QC, 512)])
        load_cols(l, kv_, [(0, O_VC, 512)])
        load_cols(l, kz, [(0, O_ZC, 512)])
        load_cols(l, kr, [(0, O_R, 32)])
        dve(lambda e: e.memset(rT[32:64, :], 0.0), [], ["rT"])
        dve(lambda e: e.memset(rT[32:33, :], 1.0), ["rT"], ["rT"])
        for (t0, n) in TBLK:
            for kc in range(8):
                pe(lambda e, t0=t0, n=n, kc=kc: e.matmul(ps[7][0:32, 0:n], lhsT=wsl[kr][:, kc, 0:32], rhs=uT[:, kc, t0:t0 + n],
                                                         start=(kc == 0), stop=(kc == 7)), WS(kr) + uT_bufs(t0, n), [P(7)])
            act(lambda e, t0=t0, n=n: e.activation(func=AF.Identity, scale=1.0, out=rT[0:32, t0:t0 + n], in_=ps[7][0:32, 0:n]), [P(7)], ["rT"])

        for dr in (1, 0):
            order = ([1, 0] + list(range(NT - 1, 1, -1))) if dr == 1 else list(range(NT))
            dve(lambda e: e.memset(Sst[:], 0.0), [], ["Sst"])
            dve(lambda e: e.memset(Sbf[:], 0.0), [], ["Sbf"])
            dve(lambda e: e.memset(qm[:], 0.0), [], ["qm"])
            for tt in order:
                tk = slice(tt * 128, (tt + 1) * 128)
                ub = [f"uT{tt}"]
                need_out = not (last and tt < 2)
                cstep(2)
                pe(lambda e, tk=tk, dr=dr: e.matmul(ps[0][:, 0:256], lhsT=rT[0:64, tk], rhs=w2c[0:64, dr * 256:(dr + 1) * 256], start=True, stop=True),
                   ["rT", "w2c"], [P(0)])
                act(lambda e: e.activation(out=lsp[:], in_=ps[0][:, 0:256], func=AF.Exp, scale=-1.0), [P(0)], ["lsp"])
                act(lambda e: e.activation(out=lsp[:], in_=lsp[:], func=AF.Ln, bias=1.0, scale=1.0), ["lsp"], ["lsp"])
                cstep(3)
                act(lambda e: e.activation(func=AF.Identity, scale=1.0, out=lhl[:, 0, :], in_=lsp[:]), ["lsp"], ["lhl"])
                dve(lambda e: e.tensor_tensor(out=lhl[:, 1, :], in0=lsp[:], in1=lhl[:, 0, :], op=ALU.subtract), ["lsp", "lhl"], ["lhl"])
                for q_ in range(2):
                    pe(lambda e, dr=dr, q_=q_: e.matmul(ps[0][:, 256:512], lhsT=trib[:, dr, :], rhs=lhl[:, q_, :], start=(q_ == 0), stop=(q_ == 1)), ["lhl", "trib"], [P(0)])
                for p in range(2):
                    for q_ in range(2):
                        pe(lambda e, p=p, q_=q_: e.matmul(ps[7][:, 2 * p:2 * p + 2], lhsT=lhl[:, q_, p * 128:(p + 1) * 128], rhs=ones_b[:, 0:2], start=(q_ == 0), stop=(q_ == 1)),
                           ["lhl", "cstb"], [P(7)])
                act(lambda e: e.activation(out=eq[:], in_=ps[0][:, 256:512], func=AF.Exp, scale=-1.0 / 16), [P(0)], ["eq"])
                act(lambda e: e.activation(out=ek[:], in_=ps[0][:, 256:512], func=AF.Exp, scale=1.0 / 16), [P(0)], ["ek"])
                act(lambda e: e.activation(out=dec[:], in_=ps[7][:, 0:4].rearrange("p (a b) -> p a b", b=2)[:, :, 0], func=AF.Exp, scale=-1.0 / 16), [P(7)], ["dec"])
                cstep(4)
                for kc in range(8):
                    pe(lambda e, tk=tk, kc=kc: e.matmul(ps[1][:], lhsT=uT[:, kc, tk], rhs=wsl[kqk][:, kc, :], start=(kc == 0), stop=(kc == 7)),
                       ub + WS(kqk), [P(1)])
                for kc in range(8):
                    pe(lambda e, tk=tk, kc=kc: e.matmul(ps[2][:], lhsT=uT[:, kc, tk], rhs=wsl[kv_][:, kc, :], start=(kc == 0), stop=(kc == 7)),
                       ub + WS(kv_), [P(2)])
                act(lambda e: e.activation(func=AF.Identity, scale=1.0, out=vbf[:], in_=ps[2][:]), [P(2)], ["vbf"])
                dve(lambda e: e.scalar_tensor_tensor(out=qs[:], in0=ps[1][:, 0:256], scalar=0.125, in1=eq[:], op0=ALU.mult, op1=ALU.mult),
                    [P(1), "eq"], ["qs"])
                dve(lambda e: e.tensor_tensor(out=kst[:], in0=ps[1][:, 256:512], in1=ek[:], op=ALU.mult), [P(1), "ek"], ["kst"])
                act(lambda e: e.activation(func=AF.Identity, scale=1.0, out=ktb[:], in_=kst[:]), ["kst"], ["ktb"])
                cstep(5)
                if need_out:
                    for p in range(2):
                        pe(lambda e, p=p: e.transpose(out=ps[3][:, p * 128:(p + 1) * 128], in_=qs[:, p * 128:(p + 1) * 128], identity=ident), ["qs", "cst"], [P(3)])
                    for p in range(2):
                        pe(lambda e, p=p: e.transpose(out=ps[3][:, (2 + p) * 128:(3 + p) * 128], in_=kst[:, p * 128:(p + 1) * 128], identity=ident), ["kst", "cst"], [P(3)])
                    act(lambda e: e.activation(func=AF.Identity, scale=1.0, out=qkT[:, 2:4, :], in_=ps[3][:, 256:512].rearrange("p (a t) -> p a t", a=2)), [P(3)], ["qkT"])
                    for hh in range(2):
                        r_ = slice(hh * 64, (hh + 1) * 64)
                        act(lambda e, hh=hh, r_=r_: e.activation(func=AF.Identity, scale=1.0, out=qm[r_, hh::2, :],
                                                                 in_=ps[3][r_, 0:256].rearrange("p (a t) -> p a t", a=2)), [P(3)], ["qm"])
                    for h in range(4):
                        p = h // 2
                        pe(lambda e, h=h, p=p: e.matmul(ps[4][:, h * 128:(h + 1) * 128], lhsT=qkT[:, 2 + p, :], rhs=qm[:, h, :],
                                                        start=True, stop=True), ["qkT", "qm"], [P(4)])
                    dve(lambda e, dr=dr: e.tensor_tensor(out=Pm[:], in0=ps[4][:].rearrange("p (h t) -> p h t", h=4), in1=mask4[:, dr, :, :], op=ALU.mult),
                        [P(4), "mask4"], ["Pm"])
                    cstep(6)
                    for h in range(4):
                        p, b0 = h // 2, 64 * (h % 2)
                        pe(lambda e, h=h: e.matmul(ps[5][:, h * 128:(h + 1) * 128], lhsT=Pm[:, h, :], rhs=vbf[:, h * 128:(h + 1) * 128], start=True, stop=False),
                           ["Pm", "vbf"], [P(5)])
                        pe(lambda e, h=h, p=p: e.matmul(ps[5][:, h * 128:(h + 1) * 128], lhsT=qm[:, h, :], rhs=Sbf[:, p, :],
                                                        start=False, stop=True), ["qm", "Sbf"], [P(5)])
                cstep(7)
                for p in range(2):
                    pe(lambda e, p=p: e.matmul(ps[6][:, p * 256:(p + 1) * 256], lhsT=ktb[:, p * 128:(p + 1) * 128], rhs=vbf[:, p * 256:(p + 1) * 256],
                                               start=True, stop=True), ["ktb", "vbf"], [P(6)])
                for hh in range(2):
                    r_ = slice(hh * 64, (hh + 1) * 64)
                    dve(lambda e, r_=r_, hh=hh: e.tensor_tensor(out=tmpS[r_, :, :], in0=ps[6][r_, :].rearrange("p (a b d) -> p a b d", a=2, b=2)[:, :, hh, :],
                                                                in1=Sst[r_, :, :], op=ALU.add), [P(6), "Sst"], ["tmpS"])
                    dve(lambda e, r_=r_: e.tensor_tensor(out=Sst[r_, :, :], in0=tmpS[r_, :, :], in1=dec[r_, :].unsqueeze(2).to_broadcast([64, 2, 128]), op=ALU.mult),
                        ["tmpS", "dec"], ["Sst"])
                act(lambda e: e.activation(func=AF.Identity, scale=1.0, out=Sbf[:], in_=Sst[:]), ["Sst"], ["Sbf"])
                cstep(8)
                if not need_out:
                    continue
                if dr == 1:
                    act(lambda e, tt=tt: e.activation(func=AF.Identity, scale=1.0, out=ostore[:, tt, :], in_=ps[5][:]), [P(5)], [f"os{tt}"])
                    continue
                dve(lambda e, tt=tt: e.tensor_tensor(out=osum[:], in0=ps[5][:], in1=ostore[:, tt, :], op=ALU.add), [P(5), f"os{tt}"], ["osum"])
                act(lambda e: e.activation(out=sq[:], in_=osum[:], func=AF.Square), ["osum"], ["sq"])
                dve(lambda e: e.tensor_reduce(out=ss[:, 0:4], in_=sq[:].rearrange("p (h d) -> p h d", h=4), axis=AX.X, op=ALU.add), ["sq"], ["ss"])
                rsqrt_act(ss[:, 0:4], ss[:, 0:4], 1.0 / 128, ["ss"], ["ss"])
                for kc in range(8):
                    pe(lambda e, tk=tk, kc=kc: e.matmul(ps[2][:], lhsT=uT[:, kc, tk], rhs=wsl[kz][:, kc, :], start=(kc == 0), stop=(kc == 7)),
                       ub + WS(kz), [P(2)])
                act(lambda e: e.activation(out=zs[:], in_=ps[2][:], func=AF.Silu), [P(2)], ["zs"])
                for h in range(4):
                    hs = slice(h * 128, (h + 1) * 128)
                    dve(lambda e, h=h, hs=hs: e.scalar_tensor_tensor(out=og[:, hs], in0=osum[:, hs], scalar=ss[:, h:h + 1], in1=cn[:, hs], op0=ALU.mult, op1=ALU.mult),
                        ["osum", "ss", "cn"], ["og"])
                dve(lambda e: e.tensor_tensor(out=og[:], in0=og[:], in1=zs[:], op=ALU.mult), ["og", "zs"], ["og"])
                for h in range(4):
                    pe(lambda e, h=h: e.transpose(out=ps[3][:, h * 128:(h + 1) * 128], in_=og[:, h * 128:(h + 1) * 128], identity=ident), ["og", "cst"], [P(3)])
                act(lambda e, tk=tk: e.activation(func=AF.Identity, scale=1.0, out=yT[Y_C][:, :, tk], in_=ps[3][:].rearrange("p (a t) -> p a t", a=4)), [P(3)], [f"yC{tt}"])
        yC_all = [f"yC{tt}" for tt in range(NT)]
        dump("yc", yT[Y_C][:], [128, 4, T], yC_all)
        A.top = marks["after_CD"]
        S.barrier()

        S.mute = "D" not in phases
        PADW = 2364
        SEG = {0: 15, 1: 301}
        pD = A.alloc("pD", [128, 4, PADW], BF16)
        dg = A.alloc("dg", [128, 4, 31, 128], BF16)
        hh_ = A.alloc("hh", [128, 4, 512], F32)
        hbf = A.alloc("hbf", [128, 4, 512], BF16)
        hsq = A.alloc("hsq", [128, 4, 512], BF16)
        mean = A.alloc("mean", [128, 512], F32)
        msq = A.alloc("msq", [128, 512], F32)
        rstd = A.alloc("rstd", [128, 512], F32)
        sg = A.alloc("sg", [128, 512], F32)
        t1 = A.alloc("t1", [128, 512], F32)
        t2 = A.alloc("t2", [128, 512], F32)
        zs2 = A.alloc("zs2", [128, 512], F32)
        dblocks = TBLK[1:] if last else TBLK
        dve(lambda e: e.memset(pD[:], 0.0), [], ["pD"])
        for fc in range(4):
            for j in range(31):
                dve(lambda e, fc=fc, j=j: e.tensor_scalar(out=dg[:, fc, j, :], in0=ident_b, scalar1=dw[:, fc, j:j + 1], scalar2=None, op0=ALU.mult),
                    ["cstb", "dw"], ["dg"])
        for fc in range(4):
            k = wslot()
            load_cols(l, k, [(0, O_GA + fc * 128, 128), (128, O_GT + fc * 128, 128)])
            for (t0, n) in dblocks:
                for g in range(2):
                    for kc in range(8):
                        pe(lambda e, k=k, g=g, kc=kc, t0=t0, n=n: e.matmul(ps[g][:, 0:n], lhsT=wsl[k][:, kc, g * 128:(g + 1) * 128], rhs=uT[:, kc, t0:t0 + n],
                                                                           start=(kc == 0), stop=(kc == 7)), WS(k) + uT_bufs(t0, n), [P(g)])
                act(lambda e, n=n: e.activation(out=sg[:, 0:n], in_=ps[1][:, 0:n], func=AF.Sigmoid), [P(1)], ["sg"])
                po = (SEG[0] + t0) if t0 < 256 else (SEG[1] + t0 - 256)
                dve(lambda e, fc=fc, n=n, po=po: e.tensor_tensor(out=pD[:, fc, po:po + n], in0=ps[0][:, 0:n], in1=sg[:, 0:n], op=ALU.mult), [P(0), "sg"], ["pD"])
        kzd = wslot()
        load_cols(l, kzd, [(0, O_ZD, 512)])
        for (t0, n) in dblocks:
            pin = (SEG[0] + t0 - 15) if t0 < 256 else (SEG[1] + t0 - 256 - 15)
            for fc in range(4):
                pb = fc % 2
                for j in range(31):
                    pe(lambda e, fc=fc, j=j, pb=pb, n=n, pin=pin: e.matmul(ps[pb][:, 0:n], lhsT=dg[:, fc, j, :], rhs=pD[:, fc, pin + j:pin + j + n],
                                                                           start=(j == 0), stop=(j == 30)), ["dg", "pD"], [P(pb)])
                act(lambda e, fc=fc, pb=pb, n=n: e.activation(out=hh_[:, fc, 0:n], in_=ps[pb][:, 0:n], func=AF.Identity, bias=dvec[:, 0, fc:fc + 1], scale=1.0),
                    [P(pb), "dvec"], ["hh"])
                act(lambda e, fc=fc, n=n: e.activation(func=AF.Identity, scale=1.0, out=hbf[:, fc, 0:n], in_=hh_[:, fc, 0:n]), ["hh"], ["hbf"])
                act(lambda e, fc=fc, n=n: e.activation(out=hsq[:, fc, 0:n], in_=hh_[:, fc, 0:n], func=AF.Square), ["hh"], ["hsq"])
            for fc in range(4):
                pe(lambda e, fc=fc, n=n: e.matmul(ps[2][:, 0:n], lhsT=ones_b, rhs=hbf[:, fc, 0:n], start=(fc == 0), stop=(fc == 3)), ["hbf", "cstb"], [P(2)])
            for fc in range(4):
                pe(lambda e, fc=fc, n=n: e.matmul(ps[3][:, 0:n], lhsT=ones_b, rhs=hsq[:, fc, 0:n], start=(fc == 0), stop=(fc == 3)), ["hsq", "cstb"], [P(3)])
            act(lambda e, n=n: e.activation(out=mean[:, 0:n], in_=ps[2][:, 0:n], func=AF.Identity, scale=1.0 / 512), [P(2)], ["mean"])
            dve(lambda e, n=n: e.tensor_tensor(out=msq[:, 0:n], in0=mean[:, 0:n], in1=mean[:, 0:n], op=ALU.mult), ["mean"], ["msq"])
            dve(lambda e, n=n: e.scalar_tensor_tensor(out=rstd[:, 0:n], in0=ps[3][:, 0:n], scalar=1.0 / 512, in1=msq[:, 0:n], op0=ALU.mult, op1=ALU.subtract),
                [P(3), "msq"], ["rstd"])
            rsqrt_act(rstd[:, 0:n], rstd[:, 0:n], 1.0, ["rstd"], ["rstd"])
            for fc in range(4):
                dve(lambda e, fc=fc, n=n: e.tensor_tensor(out=t1[:, 0:n], in0=hh_[:, fc, 0:n], in1=mean[:, 0:n], op=ALU.subtract), ["hh", "mean"], ["t1"])
                dve(lambda e, n=n: e.tensor_tensor(out=t1[:, 0:n], in0=t1[:, 0:n], in1=rstd[:, 0:n], op=ALU.mult), ["t1", "rstd"], ["t1"])
                dve(lambda e, fc=fc, n=n: e.tensor_scalar(out=t1[:, 0:n], in0=t1[:, 0:n], scalar1=dvec[:, 1, fc:fc + 1], scalar2=dvec[:, 2, fc:fc + 1],
                                                          op0=ALU.mult, op1=ALU.add), ["t1", "dvec"], ["t1"])
                act(lambda e, n=n: e.activation(out=t2[:, 0:n], in_=t1[:, 0:n], func=AF.Silu), ["t1"], ["t2"])
                for kc in range(8):
                    pe(lambda e, fc=fc, kc=kc, t0=t0, n=n: e.matmul(ps[4][:, 0:n], lhsT=wsl[kzd][:, kc, fc * 128:(fc + 1) * 128], rhs=uT[:, kc, t0:t0 + n],
                                                                    start=(kc == 0), stop=(kc == 7)), WS(kzd) + uT_bufs(t0, n), [P(4)])
                act(lambda e, n=n: e.activation(out=zs2[:, 0:n], in_=ps[4][:, 0:n], func=AF.Silu), [P(4)], ["zs2"])
                dve(lambda e, fc=fc, t0=t0, n=n: e.tensor_tensor(out=yT[Y_D][:, fc, t0:t0 + n], in0=t2[:, 0:n], in1=zs2[:, 0:n], op=ALU.mult),
                    ["t2", "zs2"], ["yD"])
        dump("yd", yT[Y_D][:], [128, 4, T], ["yD"])
        A.top = marks["after_CDA"]
        S.barrier()

        S.mute = "A" not in phases
        qT = A.alloc("qT", [128, 4, T], BF16)
        kT = A.alloc("kT", [128, 2, T], BF16)
        va = A.alloc("va", [128, NT, 256], BF16)
        rope = A.alloc("rope", [128, 2, 16, 64], F32)
        qn = A.alloc("qn", [128, 768], F32)
        qr = A.alloc("qr", [128, 768], F32)
        rt = [A.alloc(f"rt{i}", [128, 6, 64], F32) for i in range(4)]
        ssa = A.alloc("ssa", [128, 8], F32)
        pT = [A.alloc(f"pT{i}", [128, 512], BF16) for i in range(2)]
        rec = A.alloc("rec", [128, 512], F32)
        ta = A.alloc("ta", [128, 512], F32)
        S.dma("sp", lambda e: e.dma_start(out=rope[:], in_=rope_d), writes=["rope"])
        kq, kkv, kza = wslot(), wslot(), wslot()
        load_cols(l, kq, [(0, O_QA, 512)])
        load_cols(l, kkv, [(0, O_KA, 512)])
        load_cols(l, kza, [(0, O_ZA, 512)])
        for tt in range(NT):
            tk = slice(tt * 128, (tt + 1) * 128)
            ub = [f"uT{tt}"]
            need_q = not (last and tt < 2)
            if need_q:
                for kc in range(8):
                    pe(lambda e, tk=tk, kc=kc: e.matmul(ps[0][:], lhsT=uT[:, kc, tk], rhs=wsl[kq][:, kc, :], start=(kc == 0), stop=(kc == 7)), ub + WS(kq), [P(0)])
            for kc in range(8):
                pe(lambda e, tk=tk, kc=kc: e.matmul(ps[1][:], lhsT=uT[:, kc, tk], rhs=wsl[kkv][:, kc, :], start=(kc == 0), stop=(kc == 7)), ub + WS(kkv), [P(1)])
            act(lambda e, tt=tt: e.activation(func=AF.Identity, scale=1.0, out=va[:, tt, :], in_=ps[1][:, 256:512]), [P(1)], ["va"])
            if need_q:
                act(lambda e: e.activation(out=qr[:, 0:512], in_=ps[0][:], func=AF.Square), [P(0)], ["qr"])
            act(lambda e: e.activation(out=qr[:, 512:768], in_=ps[1][:, 0:256], func=AF.Square), [P(1)], ["qr"])
            h0 = 0 if need_q else 4
            dve(lambda e, h0=h0: e.tensor_reduce(out=ssa[:, h0:6], in_=qr[:, h0 * 128:768].rearrange("p (h d) -> p h d", d=128), axis=AX.X, op=ALU.add), ["qr"], ["ssa"])
            rsqrt_act(ssa[:, h0:6], ssa[:, h0:6], 1.0 / 128, ["ssa"], ["ssa"])
            for h in range(h0, 6):
                src = ps[0][:, h * 128:(h + 1) * 128] if h < 4 else ps[1][:, (h - 4) * 128:(h - 3) * 128]
                gn = qg if h < 4 else kg
                dve(lambda e, h=h, src=src, gn=gn: e.scalar_tensor_tensor(out=qn[:, h * 128:(h + 1) * 128], in0=src, scalar=ssa[:, h:h + 1], in1=gn[:], op0=ALU.mult, op1=ALU.mult),
                    [P(0) if h < 4 else P(1), "ssa", "qg", "kg"], ["qn"])
            if tt >= 2:
                tl = tt - 2
                nh = 6 - h0
                qv = qn[:, h0 * 128:768].rearrange("p (h i two) -> p h i two", i=64, two=2)
                ov = qr[:, h0 * 128:768].rearrange("p (h i two) -> p h i two", i=64, two=2)
                cb = rope[:, 0, tl, :].unsqueeze(1).to_broadcast([128, nh, 64])
                sb = rope[:, 1, tl, :].unsqueeze(1).to_broadcast([128, nh, 64])
                xe, xo = qv[:, :, :, 0], qv[:, :, :, 1]
                dve(lambda e, xe=xe, cb=cb, nh=nh: e.tensor_tensor(out=rt[0][:, 0:nh, :], in0=xe, in1=cb, op=ALU.mult), ["qn", "rope"], ["rt0"])
                dve(lambda e, xo=xo, sb=sb, nh=nh: e.tensor_tensor(out=rt[1][:, 0:nh, :], in0=xo, in1=sb, op=ALU.mult), ["qn", "rope"], ["rt1"])
                dve(lambda e, ov=ov, nh=nh: e.tensor_tensor(out=ov[:, :, :, 0], in0=rt[0][:, 0:nh, :], in1=rt[1][:, 0:nh, :], op=ALU.subtract), ["rt0", "rt1"], ["qr"])
                dve(lambda e, xe=xe, sb=sb, nh=nh: e.tensor_tensor(out=rt[2][:, 0:nh, :], in0=xe, in1=sb, op=ALU.mult), ["qn", "rope"], ["rt2"])
                dve(lambda e, xo=xo, cb=cb, nh=nh: e.tensor_tensor(out=rt[3][:, 0:nh, :], in0=xo, in1=cb, op=ALU.mult), ["qn", "rope"], ["rt3"])
                dve(lambda e, ov=ov, nh=nh: e.tensor_tensor(out=ov[:, :, :, 1], in0=rt[2][:, 0:nh, :], in1=rt[3][:, 0:nh, :], op=ALU.add), ["rt2", "rt3"], ["qr"])
                srcq, srcn = qr, "qr"
            else:
                srcq, srcn = qn, "qn"
            for h in range(h0, 6):
                pe(lambda e, h=h, srcq=srcq: e.transpose(out=ps[2 + h // 4][:, (h % 4) * 128:(h % 4 + 1) * 128], in_=srcq[:, h * 128:(h + 1) * 128], identity=ident),
                   [srcn, "cst"], [P(2 + h // 4)])
            if need_q:
                act(lambda e, tk=tk: e.activation(out=qT[:, :, tk], in_=ps[2][:].rearrange("p (a t) -> p a t", a=4), func=AF.Identity, scale=128 ** -0.5), [P(2)], ["qT"])
            act(lambda e, tk=tk: e.activation(func=AF.Identity, scale=1.0, out=kT[:, :, tk], in_=ps[3][:, 0:256].rearrange("p (a t) -> p a t", a=2)), [P(3)], ["kT"])
        ablocks = TBLK[1:] if last else TBLK
        it = 0
        for (t0, n) in ablocks:
            kts = [0, 1] if t0 < 256 else list(range(NT))
            for h in range(4):
                g = h // 2
                po, pd = 4 + (it % 2), 6 + (it % 2)
                it += 1
                for i_, kt in enumerate(kts):
                    b = i_ % 2
                    pe(lambda e, kt=kt, g=g, h=h, b=b, t0=t0, n=n: e.matmul(ps[b][:, 0:n], lhsT=kT[:, g, kt * 128:(kt + 1) * 128], rhs=qT[:, h, t0:t0 + n], start=True, stop=True),
                       ["kT", "qT"], [P(b)])
                    act(lambda e, b=b, n=n: e.activation(out=pT[b][:, 0:n], in_=ps[b][:, 0:n], func=AF.Exp), [P(b)], [f"pT{b}"])
                    pe(lambda e, kt=kt, g=g, b=b, n=n, po=po, i_=i_, kts=kts: e.matmul(ps[po][:, 0:n], lhsT=va[:, kt, g * 128:(g + 1) * 128], rhs=pT[b][:, 0:n],
                                                                                   start=(i_ == 0), stop=(i_ == len(kts) - 1)), ["va", f"pT{b}"], [P(po)])
                    pe(lambda e, b=b, n=n, pd=pd, i_=i_, kts=kts: e.matmul(ps[pd][:, 0:n], lhsT=ones_b, rhs=pT[b][:, 0:n],
                                                                         start=(i_ == 0), stop=(i_ == len(kts) - 1)), ["cstb", f"pT{b}"], [P(pd)])
                for kc in range(8):
                    pe(lambda e, kc=kc, h=h, t0=t0, n=n: e.matmul(ps[2][:, 0:n], lhsT=wsl[kza][:, kc, h * 128:(h + 1) * 128], rhs=uT[:, kc, t0:t0 + n],
                                                                  start=(kc == 0), stop=(kc == 7)), WS(kza) + uT_bufs(t0, n), [P(2)])
                act(lambda e, n=n: e.activation(out=ta[:, 0:n], in_=ps[2][:, 0:n], func=AF.Silu), [P(2)], ["ta"])
                dve(lambda e, n=n, pd=pd: e.reciprocal(out=rec[:, 0:n], in_=ps[pd][:, 0:n]), [P(pd)], ["rec"])
                dve(lambda e, n=n, po=po: e.tensor_tensor(out=rec[:, 0:n], in0=ps[po][:, 0:n], in1=rec[:, 0:n], op=ALU.mult), [P(po), "rec"], ["rec"])
                dve(lambda e, h=h, t0=t0, n=n: e.tensor_tensor(out=yT[Y_A][:, h, t0:t0 + n], in0=rec[:, 0:n], in1=ta[:, 0:n], op=ALU.mult), ["rec", "ta"], ["yA"])
        dump("ya", yT[Y_A][:], [128, 4, T], ["yA"])
        A.top = marks["after_all"]
        S.barrier()

        S.mute = "B" not in phases
        MW = 2308
        mB = A.alloc("mB", [128, MW], F32)
        cv = A.alloc("cv", [128, MW], F32)
        cs = A.alloc("cs", [128, 512], F32)
        zb = A.alloc("zb", [128, 512], F32)
        tb = A.alloc("tb", [128, 512], F32)
        bblocks = TBLK[1:] if last else TBLK
        dve(lambda e: e.memset(mB[:], 0.0), [], ["mB"])

        def mpos(t0):
            return (1 + t0) if t0 < 256 else (259 + t0 - 256)
        for fc in range(4):
            k = wslot()
            load_cols(l, k, [(0, O_CB + fc * 128, 128), (128, O_XB + fc * 128, 128), (256, O_BB + fc * 128, 128), (384, O_ZB + fc * 128, 128)])
            for (t0, n) in bblocks:
                for g in range(2):
                    for kc in range(8):
                        pe(lambda e, k=k, g=g, kc=kc, t0=t0, n=n: e.matmul(ps[g][:, 0:n], lhsT=wsl[k][:, kc, g * 128:(g + 1) * 128], rhs=uT[:, kc, t0:t0 + n],
                                                                           start=(kc == 0), stop=(kc == 7)), WS(k) + uT_bufs(t0, n), [P(g)])
                act(lambda e, n=n: e.activation(func=AF.Identity, scale=1.0, out=cs[:, 0:n], in_=ps[0][:, 0:n]), [P(0)], ["cs"])
                dve(lambda e, t0=t0, n=n: e.tensor_tensor(out=mB[:, mpos(t0):mpos(t0) + n], in0=cs[:, 0:n], in1=ps[1][:, 0:n], op=ALU.mult), ["cs", P(1)], ["mB"])
            L = MW - 2
            dve(lambda e, fc=fc, L=L: e.tensor_scalar(out=cv[:, 1:1 + L], in0=mB[:, 0:L], scalar1=bw[:, fc, 0:1], scalar2=None, op0=ALU.mult), ["mB", "bw"], ["cv"])
            dve(lambda e, fc=fc, L=L: e.scalar_tensor_tensor(out=cv[:, 1:1 + L], in0=mB[:, 1:1 + L], scalar=bw[:, fc, 1:2], in1=cv[:, 1:1 + L], op0=ALU.mult, op1=ALU.add),
                ["mB", "bw", "cv"], ["cv"])
            dve(lambda e, fc=fc, L=L: e.scalar_tensor_tensor(out=cv[:, 1:1 + L], in0=mB[:, 2:2 + L], scalar=bw[:, fc, 2:3], in1=cv[:, 1:1 + L], op0=ALU.mult, op1=ALU.add),
                ["mB", "bw", "cv"], ["cv"])
            for (t0, n) in bblocks:
                for g in range(2):
                    for kc in range(8):
                        pe(lambda e, k=k, g=g, kc=kc, t0=t0, n=n: e.matmul(ps[2 + g][:, 0:n], lhsT=wsl[k][:, kc, (2 + g) * 128:(3 + g) * 128], rhs=uT[:, kc, t0:t0 + n],
                                                                           start=(kc == 0), stop=(kc == 7)), WS(k) + uT_bufs(t0, n), [P(2 + g)])
                act(lambda e, n=n: e.activation(out=zb[:, 0:n], in_=ps[3][:, 0:n], func=AF.Silu), [P(3)], ["zb"])
                dve(lambda e, t0=t0, n=n: e.tensor_tensor(out=tb[:, 0:n], in0=ps[2][:, 0:n], in1=cv[:, mpos(t0):mpos(t0) + n], op=ALU.mult), [P(2), "cv"], ["tb"])
                dve(lambda e, fc=fc, t0=t0, n=n: e.tensor_tensor(out=yT[Y_B][:, fc, t0:t0 + n], in0=tb[:, 0:n], in1=zb[:, 0:n], op=ALU.mult), ["tb", "zb"], ["yB"])
        dump("yb", yT[Y_B][:], [128, 4, T], ["yB"])
        A.top = marks["after_all"]
        S.barrier()

        S.mute = "M" not in phases
        accT = A.alloc("accT", [128, 8, T], BF16)
        brw = [A.alloc(f"brw{i}", [128, 4, 4, 128], BF16) for i in range(2)]
        sgm = A.alloc("sgm", [128, 512], F32)
        acc = A.alloc("acc", [128, 512], F32)
        tm = A.alloc("tm", [128, 512], F32)
        ybr = [Y_A, Y_B, Y_C, Y_D]
        ynm = ["yA", "yB", "yC_all", "yD"]
        mblocks = TBLK[1:] if last else TBLK
        for dc in range(8):
            k = wslot()
            load_cols(l, k, [(i * 128, O_MG + i * 1024 + dc * 128, 128) for i in range(4)])
            bi = dc % 2
            for i in range(4):
                S.dma("pool", lambda e, i=i, bi=bi, dc=dc: e.dma_start(out=brw[bi][:, :, i, :],
                                                                      in_=w_br_d[l, i].rearrange("(kc p) n -> p kc n", p=128)[:, :, dc * 128:(dc + 1) * 128]),
                      writes=[f"brw{bi}"])
            for (t0, n) in mblocks:
                for i in range(4):
                    yb_ = yC_all if ynm[i] == "yC_all" else [ynm[i]]
                    for kc in range(8):
                        pe(lambda e, k=k, i=i, kc=kc, t0=t0, n=n: e.matmul(ps[i % 2][:, 0:n], lhsT=wsl[k][:, kc, i * 128:(i + 1) * 128], rhs=uT[:, kc, t0:t0 + n],
                                                                           start=(kc == 0), stop=(kc == 7)), WS(k) + uT_bufs(t0, n), [P(i % 2)])
                    for kc in range(4):
                        pe(lambda e, bi=bi, i=i, kc=kc, t0=t0, n=n: e.matmul(ps[2 + i % 2][:, 0:n], lhsT=brw[bi][:, kc, i, :], rhs=yT[ybr[i]][:, kc, t0:t0 + n],
                                                                             start=(kc == 0), stop=(kc == 3)), [f"brw{bi}"] + yb_, [P(2 + i % 2)])
                    act(lambda e, i=i, n=n: e.activation(out=sgm[:, 0:n], in_=ps[i % 2][:, 0:n], func=AF.Sigmoid), [P(i % 2)], ["sgm"])
                    if i == 0:
                        dve(lambda e, n=n: e.tensor_tensor(out=acc[:, 0:n], in0=sgm[:, 0:n], in1=ps[2][:, 0:n], op=ALU.mult), ["sgm", P(2)], ["acc"])
                    else:
                        dve(lambda e, i=i, n=n: e.tensor_tensor(out=tm[:, 0:n], in0=sgm[:, 0:n], in1=ps[2 + i % 2][:, 0:n], op=ALU.mult), ["sgm", P(2 + i % 2)], ["tm"])
                        if i < 3:
                            dve(lambda e, n=n: e.tensor_tensor(out=acc[:, 0:n], in0=acc[:, 0:n], in1=tm[:, 0:n], op=ALU.add), ["acc", "tm"], ["acc"])
                        else:
                            dve(lambda e, dc=dc, t0=t0, n=n: e.tensor_tensor(out=accT[:, dc, t0:t0 + n], in0=acc[:, 0:n], in1=tm[:, 0:n], op=ALU.add), ["acc", "tm"], ["accT"])
        S.barrier()

        S.mute = "E" not in phases
        A.top = marks["y0"]
        wo = A.alloc("wo", [128, 8, D], BF16)
        gbc = A.alloc("gbc", [128, 2, D], F32)
        lng = A.alloc("lng", [128, D], F32)
        lnb = A.alloc("lnb", [128, D], F32)
        dgt = A.alloc("dgt", [128, 128], F32)
        dgb = A.alloc("dgb", [128, 2, 128], BF16)
        hre = [A.alloc(f"hre{i}", [128, D], F32) for i in range(2)]
        yv = [A.alloc(f"yv{i}", [128, D], F32) for i in range(2)]
        ste = [A.alloc(f"ste{i}", [128, 16], F32) for i in range(2)]
        wov = w_out_d[l].rearrange("(kc p) n -> p kc n", p=128)
        for hf in range(2):
            S.dma("pool", lambda e, hf=hf: e.dma_start(out=wo[:, :, hf * 512:(hf + 1) * 512], in_=wov[:, :, hf * 512:(hf + 1) * 512]), writes=["wo"])
        S.dma("sp", lambda e: e.dma_start(out=lng[:], in_=lng_d[l]), writes=["lng"])
        S.dma("sp", lambda e: e.dma_start(out=lnb[:], in_=lnb_d[l]), writes=["lnb"])
        for w in range(2):
            for c in range(8):
                dve(lambda e, w=w, c=c: e.tensor_scalar(out=dgt[:], in0=ident, scalar1=modT[:, 16 + c, w:w + 1], scalar2=None, op0=ALU.mult), ["cst", "modT"], ["dgt"])
                dve(lambda e: e.tensor_copy(out=dgb[:, 0, :], in_=dgt[:]), ["dgt"], ["dgb"])
                dve(lambda e: e.tensor_tensor(out=dgb[:, 1, :], in0=dgt[:], in1=dgb[:, 0, :], op=ALU.subtract), ["dgt", "dgb"], ["dgb"])
                for q_ in range(2):
                    pe(lambda e, c=c, q_=q_: e.matmul(ps[c % 2][:, 0:128], lhsT=ones_b, rhs=dgb[:, q_, :], start=(q_ == 0), stop=(q_ == 1)), ["cstb", "dgb"], [P(c % 2)])
                act(lambda e, w=w, c=c: e.activation(func=AF.Identity, scale=1.0, out=gbc[:, w, c * 128:(c + 1) * 128], in_=ps[c % 2][:, 0:128]), [P(c % 2)], ["gbc"])
        for tt in (range(2, NT) if last else range(NT)):
            i = tt % 2
            w = 0 if tt >= 2 else 1
            tk = slice(tt * 128, (tt + 1) * 128)
            S.dma("sp", lambda e, i=i, tt=tt: e.dma_start(out=hre[i][:], in_=hsrc(tt)), reads=[f"hb{tt}"], writes=[f"hre{i}"])
            for hf in range(2):
                pb = 2 + i * 2 + hf
                for kc in range(8):
                    pe(lambda e, kc=kc, hf=hf, pb=pb, tk=tk: e.matmul(ps[pb][:], lhsT=accT[:, kc, tk], rhs=wo[:, kc, hf * 512:(hf + 1) * 512], start=(kc == 0), stop=(kc == 7)),
                       ["accT", "wo"], [P(pb)])
                dve(lambda e, i=i, hf=hf, pb=pb, w=w: e.tensor_tensor(out=yv[i][:, hf * 512:(hf + 1) * 512], in0=ps[pb][:], in1=gbc[:, w, hf * 512:(hf + 1) * 512], op=ALU.mult),
                    [P(pb), "gbc"], [f"yv{i}"])
            dve(lambda e, i=i: e.scalar_tensor_tensor(out=yv[i][:], in0=hre[i][:], scalar=float(ALPHA), in1=yv[i][:], op0=ALU.mult, op1=ALU.add), [f"hre{i}", f"yv{i}"], [f"yv{i}"])
            for hh in range(2):
                dve(lambda e, i=i, hh=hh: e.bn_stats(out=ste[i][:, hh * 6:(hh + 1) * 6], in_=yv[i][:, hh * 512:(hh + 1) * 512]), [f"yv{i}"], [f"ste{i}"])
            dve(lambda e, i=i: e.bn_aggr(out=ste[i][:, 12:14], in_=ste[i][:, 0:12]), [f"ste{i}"], [f"ste{i}"])
            rsqrt_act(ste[i][:, 14:15], ste[i][:, 13:14], 1.0, [f"ste{i}"], [f"ste{i}"])
            dve(lambda e, i=i: e.tensor_scalar(out=yv[i][:], in0=yv[i][:], scalar1=ste[i][:, 12:13], scalar2=ste[i][:, 14:15], op0=ALU.subtract, op1=ALU.mult),
                [f"yv{i}", f"ste{i}"], [f"yv{i}"])
            dve(lambda e, i=i: e.tensor_tensor(out=yv[i][:], in0=yv[i][:], in1=lng[:], op=ALU.mult), [f"yv{i}", "lng"], [f"yv{i}"])
            dve(lambda e, i=i: e.tensor_tensor(out=hre[i][:], in0=yv[i][:], in1=lnb[:], op=ALU.add), [f"yv{i}", "lnb"], [f"hre{i}"])
            if last:
                S.dma("sp", lambda e, i=i, tt=tt: e.dma_start(out=out_d[(tt - 2) * 128:(tt - 1) * 128, :], in_=hre[i][:]), reads=[f"hre{i}"], writes=[f"out{tt}"])
            elif n_layers == 1:
                if tt >= 2:
                    S.dma("sp", lambda e, i=i, tt=tt: e.dma_start(out=out_d[(tt - 2) * 128:(tt - 1) * 128, :], in_=hre[i][:]), reads=[f"hre{i}"], writes=[f"out{tt}"])
            else:
                S.dma("sp", lambda e, i=i, tt=tt: e.dma_start(out=hbuf[tt * 128:(tt + 1) * 128, :], in_=hre[i][:]), reads=[f"hre{i}"], writes=[f"hb{tt}"])

    for l_ in range(n_layers):
        emit_layer(l_)

    S.mute = False
    S.barrier()
    S.op("sp", lambda e: e.nop())
    counts = S.finalize_and_emit()
    return nc, counts, A.peak


def _consts():
    cst = np.zeros((128, 5, 128), np.float32)
    cst[:, 0, :] = np.eye(128, dtype=np.float32)
    cst[:, 1, :] = 1.0
    j = np.arange(128)[:, None]
    i = np.arange(128)[None, :]
    cst[:, 2, :] = (j <= i)
    cst[:, 3, :] = (j >= i)
    rows = SEQ // 64
    row = np.repeat(np.arange(rows), 64).astype(np.float32)
    col = np.tile(np.arange(64), rows).astype(np.float32)
    inv = (np.float32(10000.0) ** (-np.arange(0, 64, 2, dtype=np.float32) / np.float32(64))).astype(np.float32)
    ang = np.concatenate([row[:, None] * inv, col[:, None] * inv], -1).astype(np.float32)
    cs = np.stack([np.cos(ang), np.sin(ang)], 0).astype(np.float32)
    rope = np.ascontiguousarray(cs.reshape(2, 16, 128, 64).transpose(2, 0, 1, 3))
    return cst, rope


def _layout_inputs(inp):
    f = lambda a: np.ascontiguousarray(np.asarray(a, dtype=np.float32))
    cst, rope = _consts()
    sh = {}
    sh["w_mod"] = f(inp["w_mod"])
    sh["bmod"] = f(inp["b_mod"].reshape(DEPTH, 24, 128).transpose(0, 2, 1))
    sh["w_in"] = f(inp["w_in"])
    sh["qg"] = f(np.broadcast_to(inp["q_norm"][:, None, :], (DEPTH, 128, 128)))
    sh["kg"] = f(np.broadcast_to(inp["k_norm"][:, None, :], (DEPTH, 128, 128)))
    sh["bw"] = f(inp["b_conv"].reshape(DEPTH, 3, 4, 128).transpose(0, 3, 2, 1))
    w2c = np.zeros((DEPTH, 128, 512), np.float32)
    w2c[:, 0:16, 0:256] = inp["c_gate_w2"][:, 0]
    w2c[:, 16:32, 256:512] = inp["c_gate_w2"][:, 1]
    w2c[:, 32, 0:256] = inp["c_gate_b"][:, 0]
    w2c[:, 32, 256:512] = inp["c_gate_b"][:, 1]
    sh["w2c"] = w2c
    sh["cn"] = f(np.broadcast_to(inp["c_norm"][:, None, :], (DEPTH, 128, 512)))
    sh["dw"] = f(inp["d_conv_w"].reshape(DEPTH, 31, 4, 128).transpose(0, 3, 2, 1))
    dv = np.stack([inp["d_conv_b"], inp["d_norm_g"], inp["d_norm_b"]], 1)
    sh["dvec"] = f(dv.reshape(DEPTH, 3, 4, 128).transpose(0, 3, 1, 2))
    sh["w_br"] = f(inp["w_br"])
    sh["w_out"] = f(inp["w_out"])
    sh["lng"] = f(np.broadcast_to(inp["ln_g"][:, None, :], (DEPTH, 128, D)))
    sh["lnb"] = f(np.broadcast_to(inp["ln_b"][:, None, :], (DEPTH, 128, D)))
    sh["cst"] = cst
    sh["rope"] = rope
    maps = []
    for b in range(8):
        m = dict(sh)
        m["x"] = f(inp["x"][b])
        m["ctx"] = f(inp["ctx"][b])
        cpk = np.stack([np.asarray(inp["c"][b]).reshape(8, 128).T, np.asarray(inp["c_ctx"]).reshape(8, 128).T], -1)
        m["cpk"] = f(cpk)
        maps.append(m)
    return maps


_PROG = {}


def kernel(**inputs):
    inp = {k: np.asarray(v) for k, v in inputs.items()}
    maps = _layout_inputs(inp)
    if "nc" not in _PROG:
        _PROG["nc"] = build_program()[0]
    res = run_bass_kernel_spmd(_PROG["nc"], maps, core_ids=list(range(8)))
    out = np.stack([np.asarray(r["out"]) for r in res.results], 0).astype(np.float32)
    return out
```
